# Optimizing a Trainium2 kernel written in Bass

```python
import jax, jax.numpy as jnp
from jax import lax
import numpy as np

D_MODEL = 2048
BATCH = 2
SEQ = 4096
DEPTH = 1

CHUNK = 64
Q_BLOCK = 128
D_MIX = D_MODEL
HGRN_WIDTH = D_MIX // 2
HGRN_DK = 128
HGRN_HEADS = HGRN_WIDTH // HGRN_DK
HGRN_DV = HGRN_WIDTH // HGRN_HEADS
FOX_WIDTH = D_MIX - HGRN_WIDTH
FOX_HEAD_DIM = 128
FOX_HEADS = FOX_WIDTH // FOX_HEAD_DIM
D_FF = ((8 * D_MODEL // 3 + 255) // 256) * 256
CONV_WIDTH = 3
EPS = 1e-6
FOX_GATE_BIAS_OFFSET = 2.0
IN_SIZES = (HGRN_HEADS * HGRN_DK, HGRN_HEADS * HGRN_DK, HGRN_WIDTH, HGRN_WIDTH,
            FOX_WIDTH, FOX_WIDTH, FOX_WIDTH, FOX_HEADS)
D_IN = sum(IN_SIZES)

kernel_name = "hymba_hgrn2_fox_convffn_block"


def rmsnorm(x, gain):
    x32 = x.astype(jnp.float32)
    y = x32 * lax.rsqrt(jnp.mean(x32 * x32, axis=-1, keepdims=True) + EPS) * gain.astype(jnp.float32)
    return y.astype(x.dtype)


def gla_chunkwise(q, k, v, logf):
    B, H, L, DK = q.shape
    DV = v.shape[-1]
    N = L // CHUNK
    r = lambda t: t.reshape(B, H, N, CHUNK, t.shape[-1])
    q, k, v, logf = r(q), r(k), r(v), r(logf)
    b = jnp.cumsum(logf, axis=-2)
    b_last = b[..., -1:, :]
    q_dec = q * jnp.exp(b)
    k_dec = k * jnp.exp(-b)
    k_end = k * jnp.exp(b_last - b)
    mask = jnp.tril(jnp.ones((CHUNK, CHUNK), dtype=bool))
    attn = jnp.einsum('bhnck,bhnsk->bhncs', q_dec, k_dec)
    attn = jnp.where(mask, attn, 0.0)
    o_intra = jnp.einsum('bhncs,bhnsv->bhncv', attn, v)
    chunk_decay = jnp.exp(b_last[..., 0, :])

    def step(S, xs):
        qd, ke, vv, dc = xs
        o = jnp.einsum('bhck,bhkv->bhcv', qd, S)
        S = dc[..., None] * S + jnp.einsum('bhck,bhcv->bhkv', ke, vv)
        return S, o

    xs = (jnp.moveaxis(q_dec, 2, 0), jnp.moveaxis(k_end, 2, 0),
          jnp.moveaxis(v, 2, 0), jnp.moveaxis(chunk_decay, 2, 0))
    S0 = jnp.zeros((B, H, DK, DV), jnp.float32)
    _, o_inter = lax.scan(step, S0, xs)
    o = o_intra + jnp.moveaxis(o_inter, 0, 2)
    return o.reshape(B, H, L, DV)


def hgrn2_mixer(q, f_logit, inp, g, lb, norm_gain):
    B, L, _ = q.shape
    dt = q.dtype
    heads = lambda t, d: t.reshape(B, L, HGRN_HEADS, d).transpose(0, 2, 1, 3).astype(jnp.float32)
    qh = heads(q, HGRN_DK) * (HGRN_DK ** -0.5)
    zh = heads(f_logit, HGRN_DK)
    vh = heads(inp, HGRN_DV)
    lbh = lb.astype(jnp.float32).reshape(HGRN_HEADS, HGRN_DK)[None, :, None, :]
    f = lbh + (1.0 - lbh) * jax.nn.sigmoid(zh)
    kh = 1.0 - f
    o = gla_chunkwise(qh, kh, vh, jnp.log(f))
    o = o * lax.rsqrt(jnp.mean(o * o, axis=-1, keepdims=True) + EPS) * norm_gain.astype(jnp.float32)
    o = o * jax.nn.silu(heads(g, HGRN_DV))
    return o.transpose(0, 2, 1, 3).reshape(B, L, HGRN_WIDTH).astype(dt)


def fox_mixer(q, k, v, f_logit, gate_bias):
    B, L, _ = q.shape
    dt = q.dtype
    heads = lambda t: t.reshape(B, L, FOX_HEADS, FOX_HEAD_DIM).transpose(0, 2, 1, 3)
    qh, kh, vh = heads(q), heads(k), heads(v)
    logf = jax.nn.log_sigmoid(f_logit.astype(jnp.float32) + gate_bias.astype(jnp.float32))
    c = jnp.cumsum(logf, axis=1).transpose(0, 2, 1)
    scale = FOX_HEAD_DIM ** -0.5
    kpos = jnp.arange(L)

    def one_block(blk):
        start = blk * Q_BLOCK
        qb = lax.dynamic_slice_in_dim(qh, start, Q_BLOCK, axis=2)
        cb = lax.dynamic_slice_in_dim(c, start, Q_BLOCK, axis=2)
        s = jnp.einsum('bhqd,bhkd->bhqk', qb, kh).astype(jnp.float32) * scale
        s = s + cb[..., :, None] - c[..., None, :]
        qpos = start + jnp.arange(Q_BLOCK)
        s = jnp.where(kpos[None, :] <= qpos[:, None], s, -jnp.inf)
        p = jax.nn.softmax(s, axis=-1)
        return jnp.einsum('bhqk,bhkd->bhqd', p.astype(vh.dtype), vh)

    out = lax.map(one_block, jnp.arange(L // Q_BLOCK))
    out = out.transpose(1, 0, 3, 2, 4).reshape(B, L, FOX_WIDTH)
    return out.astype(dt)


def conv_ffn(h, w_gate, w_up, conv_w, conv_b, w_down):
    L = h.shape[1]
    a = h @ w_gate
    a_pad = jnp.pad(a, ((0, 0), (CONV_WIDTH - 1, 0), (0, 0)))
    a = sum(conv_w[j] * a_pad[:, j:j + L] for j in range(CONV_WIDTH)) + conv_b
    return (jax.nn.silu(a) * (h @ w_up)) @ w_down


def setup_inputs(seed: int = 0) -> dict:
    key = jax.random.key(seed)
    ks = jax.random.split(key, 16)
    nrm = lambda k, shape, s: jax.random.normal(k, shape, jnp.float32) * s
    return {
        'x': nrm(ks[0], (BATCH, SEQ, D_MODEL), 1.0),
        'norm1_gain': 1.0 + nrm(ks[1], (DEPTH, D_MODEL), 0.02),
        'w_in': nrm(ks[2], (DEPTH, D_MODEL, D_IN), D_MODEL ** -0.5),
        'hgrn_lb_param': nrm(ks[3], (DEPTH + 1, HGRN_HEADS * HGRN_DK), 0.5),
        'hgrn_norm_gain': 1.0 + nrm(ks[4], (DEPTH, HGRN_DV), 0.02),
        'fox_gate_bias': FOX_GATE_BIAS_OFFSET + nrm(ks[5], (DEPTH, FOX_HEADS), 0.1),
        'w_out': nrm(ks[6], (DEPTH, D_MIX, D_MODEL), D_MIX ** -0.5),
        'norm2_gain': 1.0 + nrm(ks[7], (DEPTH, D_MODEL), 0.02),
        'w_ffn_gate': nrm(ks[8], (DEPTH, D_MODEL, D_FF), D_MODEL ** -0.5),
        'w_ffn_up': nrm(ks[9], (DEPTH, D_MODEL, D_FF), D_MODEL ** -0.5),
        'ffn_conv_w': nrm(ks[10], (DEPTH, CONV_WIDTH, D_FF), CONV_WIDTH ** -0.5),
        'ffn_conv_b': nrm(ks[11], (DEPTH, D_FF), 0.02),
        'w_ffn_down': nrm(ks[12], (DEPTH, D_FF, D_MODEL), D_FF ** -0.5),
        'final_norm_gain': 1.0 + nrm(ks[13], (D_MODEL,), 0.02),
    }


def reference(x, norm1_gain, w_in, hgrn_lb_param, hgrn_norm_gain, fox_gate_bias, w_out,
              norm2_gain, w_ffn_gate, w_ffn_up, ffn_conv_w, ffn_conv_b, w_ffn_down,
              final_norm_gain):
    split_points = [int(v) for v in np.cumsum(IN_SIZES)[:-1]]
    lb_all = jnp.cumsum(jax.nn.softmax(hgrn_lb_param.astype(jnp.float32), axis=0), axis=0)
    for l in range(DEPTH):
        h = rmsnorm(x, norm1_gain[l])
        proj = h @ w_in[l]
        q_a, f_a, i_a, g_a, q_b, k_b, v_b, f_b = jnp.split(proj, split_points, axis=-1)
        o_a = hgrn2_mixer(q_a, f_a, i_a, g_a, lb_all[l], hgrn_norm_gain[l])
        o_b = fox_mixer(q_b, k_b, v_b, f_b, fox_gate_bias[l])
        x = x + jnp.concatenate([o_a, o_b], axis=-1) @ w_out[l]
        h = rmsnorm(x, norm2_gain[l])
        x = x + conv_ffn(h, w_ffn_gate[l], w_ffn_up[l], ffn_conv_w[l], ffn_conv_b[l], w_ffn_down[l])
    return rmsnorm(x, final_norm_gain)
```

```python
import contextlib
import numpy as np
import ml_dtypes
import concourse.bass as bass
import concourse.mybir as mybir
from concourse.bass_utils import run_bass_kernel_spmd

F32 = mybir.dt.float32
BF16 = mybir.dt.bfloat16
AF = mybir.ActivationFunctionType
ALU = mybir.AluOpType

D = 2048
L = 4096
DFF = 5632
NFC = DFF // 128
EPS = 1e-6
TB = 1026
TT = 342

ENGS = ("sp", "act", "pool", "pe", "dve")


class Sched:
    def __init__(self, nc, stack):
        self.nc = nc
        self.stack = stack
        self.ops = {e: [] for e in ENGS}
        self.res = {}
        self.esem = {}
        self.dsems = {}
        self.dcount = {}
        for e in ("act", "pool", "pe", "dve"):
            self.esem[e] = stack.enter_context(nc.semaphore("es_" + e))

    def _deps(self, eng, reads, writes):
        deps = set()
        for r in reads:
            st = self.res.get(r)
            if st is not None and st["w"] is not None:
                deps.add(st["w"])
            if st is not None and r.startswith("ps"):
                for rd in st["r"]:
                    if rd[0] != eng:
                        deps.add(rd)
        for w in writes:
            st = self.res.get(w)
            if st is not None:
                if st["w"] is not None:
                    deps.add(st["w"])
                for rd in st["r"]:
                    deps.add(rd)
        return deps

    def op(self, eng, fn, reads=(), writes=(), dma_slot=None):
        idx = len(self.ops[eng])
        deps = self._deps(eng, reads, writes)
        rec = dict(fn=fn, deps=deps, dma=dma_slot is not None, need_inc=False)
        if dma_slot is not None:
            if dma_slot not in self.dsems:
                self.dsems[dma_slot] = self.stack.enter_context(
                    self.nc.semaphore("ds_" + dma_slot.replace(".", "_")))
                self.dcount[dma_slot] = 0
            self.dcount[dma_slot] += 1
            rec["dsem"] = self.dsems[dma_slot]
            rec["dval"] = 16 * self.dcount[dma_slot]
        self.ops[eng].append(rec)
        for r in reads:
            self.res.setdefault(r, {"w": None, "r": []})["r"].append((eng, idx))
        for w in writes:
            self.res[w] = {"w": (eng, idx), "r": []}
        return (eng, idx)

    def wait_all(self, eng, deps):
        rec = dict(fn=None, deps=set(deps), dma=False, need_inc=False)
        self.ops[eng].append(rec)

    def barrier(self):
        last = {e: (e, len(self.ops[e]) - 1) for e in ENGS if self.ops[e]}
        for e in ENGS:
            self.wait_all(e, [v for k, v in last.items() if k != e])

    def emit(self, block):
        for eng in ENGS:
            for rec in self.ops[eng]:
                for (pe_, pi) in rec["deps"]:
                    p = self.ops[pe_][pi]
                    if not p["dma"]:
                        p["need_inc"] = True
        for eng in ENGS:
            cnt = 0
            for rec in self.ops[eng]:
                if rec["need_inc"] and not rec["dma"]:
                    cnt += 1
                rec["cnt"] = cnt
        names = dict(sp="sync", act="scalar", pool="gpsimd", pe="tensor", dve="vector")
        for eng in ENGS:
            if not self.ops[eng]:
                continue

            def body(e, eng=eng):
                waited = {}
                for rec in self.ops[eng]:
                    for dep in sorted(rec["deps"]):
                        p = self.ops[dep[0]][dep[1]]
                        if p["dma"]:
                            sem, val, key = p["dsem"], p["dval"], ("d", id(p["dsem"]))
                        else:
                            if dep[0] == eng and eng == "pe":
                                continue
                            sem, val, key = self.esem[dep[0]], p["cnt"], ("e", dep[0])
                        if waited.get(key, 0) >= val:
                            continue
                        e.wait_ge(sem, val)
                        waited[key] = val
                    if rec["fn"] is None:
                        continue
                    inst = rec["fn"](e)
                    if rec["dma"]:
                        inst.then_inc(rec["dsem"], 16)
                    elif rec["need_inc"]:
                        inst.then_inc(self.esem[eng], 1)

            getattr(block, names[eng])(body)


class Ctx:
    def __init__(self, S):
        self.S = S

    def dma(self, eng, out, in_, slot, r=(), w=()):
        return self.S.op(eng, lambda e: e.dma_start(out=out, in_=in_), reads=r, writes=w,
                         dma_slot=slot)

    def mm(self, out, lhsT, rhs, start, stop, r, w):
        return self.S.op("pe", lambda e: e.matmul(out, lhsT, rhs, start=start, stop=stop),
                         reads=r, writes=w)

    def tr(self, out, in_, ident, r, w):
        return self.S.op("pe", lambda e: e.transpose(out, in_, ident), reads=r, writes=w)

    def act(self, out, in_, func, r, w, bias=None, scale=None, eng="act"):
        kw = {}
        if bias is not None:
            kw["bias"] = bias
        if scale is not None:
            kw["scale"] = scale
        return self.S.op("act", lambda e: e.activation(out, in_, func, **kw), reads=r, writes=w)

    def ts(self, eng, out, in0, s1, s2, op0, op1, r, w):
        if op1 is None:
            return self.S.op(eng, lambda e: e.tensor_scalar(out, in0, s1, None, op0),
                             reads=r, writes=w)
        return self.S.op(eng, lambda e: e.tensor_scalar(out, in0, s1, s2, op0, op1),
                         reads=r, writes=w)

    def stt(self, out, in0, scalar, in1, op0, op1, r, w):
        return self.S.op("dve", lambda e: e.scalar_tensor_tensor(out, in0, scalar, in1, op0, op1),
                         reads=r, writes=w)

    def tt(self, eng, out, in0, in1, op, r, w):
        return self.S.op(eng, lambda e: e.tensor_tensor(out, in0, in1, op), reads=r, writes=w)

    def scan(self, out, d0, d1, initial, op0, op1, r, w):
        return self.S.op("dve", lambda e: e.tensor_tensor_scan(out, d0, d1, initial, op0, op1),
                         reads=r, writes=w)

    def recip(self, out, in_, r, w):
        return self.S.op("dve", lambda e: e.reciprocal(out, in_), reads=r, writes=w)

    def copy(self, eng, out, in_, r, w):
        if eng == "act":
            return self.S.op("act", lambda e: e.activation(out, in_, AF.Copy), reads=r, writes=w)
        return self.S.op(eng, lambda e: e.tensor_copy(out, in_), reads=r, writes=w)

    def memset(self, eng, ap, val, w):
        return self.S.op(eng, lambda e: e.memset(ap, val), reads=(), writes=w)


def rstd_from_ss(C, ps_ap, tmp_ap, out_ap, inv_n, r, wt, wo):
    C.ts("dve", tmp_ap, ps_ap, inv_n, EPS, ALU.mult, ALU.add, r=r, w=[wt])
    C.act(tmp_ap, tmp_ap, AF.Sqrt, r=[wt], w=[wt])
    C.recip(out_ap, tmp_ap, r=[wt], w=[wo])


def build_B():
    nc = bass.Bass("TRN2", target_bir_lowering=False)
    xT = nc.dram_tensor("xT", [D, TB], F32, kind="ExternalInput").ap()
    oT = nc.dram_tensor("oT", [D, TB], BF16, kind="ExternalInput").ap()
    w_out = nc.dram_tensor("w_out", [D, D], F32, kind="ExternalInput").ap()
    g2d = nc.dram_tensor("g2", [128, 16], F32, kind="ExternalInput").ap()
    gfd = nc.dram_tensor("gf", [128, 16], F32, kind="ExternalInput").ap()
    w_gate = nc.dram_tensor("w_gate", [D, DFF], F32, kind="ExternalInput").ap()
    w_up = nc.dram_tensor("w_up", [D, DFF], F32, kind="ExternalInput").ap()
    w_down = nc.dram_tensor("w_down", [DFF, D], F32, kind="ExternalInput").ap()
    cwd = nc.dram_tensor("cw", [128, NFC * 3], F32, kind="ExternalInput").ap()
    cbd = nc.dram_tensor("cb", [128, NFC], F32, kind="ExternalInput").ap()
    onesd = nc.dram_tensor("ones", [128, 128], BF16, kind="ExternalInput").ap()
    yT = nc.dram_tensor("yT", [D, 1024], F32, kind="ExternalOutput").ap()

    xT_v = xT.rearrange("(k p) t -> p k t", p=128)
    oT_v = oT.rearrange("(k p) t -> p k t", p=128)
    yT_v = yT.rearrange("(k p) t -> p k t", p=128)
    wo_v = w_out.rearrange("(k p) d -> p k d", p=128)
    wg_v = w_gate.rearrange("(k p) f -> p k f", p=128)
    wu_v = w_up.rearrange("(k p) f -> p k f", p=128)
    wd_v = w_down.rearrange("(c p) d -> p c d", p=128)

    with contextlib.ExitStack() as st:
        sb = lambda name, shape, dt: st.enter_context(nc.sbuf_tensor(name, shape, dt))
        x1 = sb("x1", [128, 16, TB], F32)
        h2 = sb("h2", [128, 16, TB], BF16)
        wb = [sb(f"wb{i}", [128, 16, 512], BF16) for i in range(2)]
        wd = [sb(f"wd{i}", [128, 4 * 2048], BF16) for i in range(2)]
        ub = [sb(f"ub{i}", [128, 4, 1024], BF16) for i in range(2)]
        rstd = sb("rstd", [128, TB], F32)
        rtmp = sb("rtmp", [128, TT], F32)
        abuf = [sb(f"a{i}", [128, TB], F32) for i in range(2)]
        tbuf = sb("tbuf", [128, 1024], F32)
        sbuf_ = sb("sbuf", [128, 1024], F32)
        g2 = sb("g2s", [128, 16], F32)
        gf = sb("gfs", [128, 16], F32)
        cw = sb("cws", [128, NFC * 3], F32)
        cb = sb("cbs", [128, NFC], F32)
        ones = sb("oness", [128, 128], BF16)
        ps = [st.enter_context(nc.psum_tensor(f"ps{i}", [128, 512], F32)) for i in range(8)]
        S = Sched(nc, st)
        C = Ctx(S)
        block = st.enter_context(nc.Block())

        sq_v = wd[0][:, 0:16 * TT].rearrange("p (k t) -> p k t", k=16)

        C.dma("sp", g2[:, :], g2d, "g2", w=["g2"])
        C.dma("sp", gf[:, :], gfd, "gf", w=["gf"])
        C.dma("sp", cw[:, :], cwd, "cw", w=["cw"])
        C.dma("sp", cb[:, :], cbd, "cb", w=["cb"])
        C.dma("sp", ones[:, :], onesd, "ones", w=["ones"])
        for q in range(4):
            C.dma("sp", h2[:, 4 * q:4 * q + 4, :], oT_v[:, 4 * q:4 * q + 4, :], f"h2.{q}",
                  w=[f"h2.{k}" for k in range(4 * q, 4 * q + 4)])
        for q in range(4):
            C.dma("sp", x1[:, 4 * q:4 * q + 4, :], xT_v[:, 4 * q:4 * q + 4, :], f"x1.{q}",
                  w=[f"x1.{k}" for k in range(4 * q, 4 * q + 4)])

        nbank = 0
        for dg in range(4):
            wbi = dg % 2
            for kq in range(4):
                C.dma("pool", wb[wbi][:, 4 * kq:4 * kq + 4, :],
                      wo_v[:, 4 * kq:4 * kq + 4, dg * 512:(dg + 1) * 512],
                      f"wb{wbi}.{kq}", w=[f"wb{wbi}.{kq}", f"wbu{wbi}.{kq}"])
            for dl in range(4):
                dc = dg * 4 + dl
                for tt in range(3):
                    bank = nbank % 6
                    nbank += 1
                    for k in range(16):
                        C.mm(ps[bank][:, 0:TT], wb[wbi][:, k, dl * 128:(dl + 1) * 128],
                             h2[:, k, tt * TT:(tt + 1) * TT], start=(k == 0), stop=(k == 15),
                             r=[f"wb{wbi}.{k // 4}", f"wbu{wbi}.{k // 4}", f"h2.{k}"], w=[f"ps{bank}"])
                    C.tt("dve", x1[:, dc, tt * TT:(tt + 1) * TT], ps[bank][:, 0:TT],
                         x1[:, dc, tt * TT:(tt + 1) * TT], ALU.add,
                         r=[f"ps{bank}", f"x1.{dc}"], w=[f"x1.{dc}"])

        def rmsnorm_stats(tt):
            cols = slice(tt * TT, (tt + 1) * TT)
            C.act(sq_v, x1[:, :, cols], AF.Square, r=[f"x1.{k}" for k in range(16)], w=["wd0.0", "wd0.1", "wd0.2"])
            for k in range(16):
                C.mm(ps[6][:, 0:TT], ones[:, :], sq_v[:, k, :], start=(k == 0), stop=(k == 15),
                     r=["ones", "wd0.0", "wd0.1", "wd0.2"], w=["ps6"])
            rstd_from_ss(C, ps[6][:, 0:TT], rtmp[:, :], rstd[:, cols], 1.0 / D,
                         r=["ps6"], wt="rtmp", wo=f"rstd.{tt}")

        for tt in range(3):
            cols = slice(tt * TT, (tt + 1) * TT)
            rmsnorm_stats(tt)
            for k in range(16):
                C.stt(h2[:, k, cols], x1[:, k, cols], g2[:, k:k + 1], rstd[:, cols],
                      ALU.mult, ALU.mult, r=[f"x1.{k}", "g2", f"rstd.{tt}"], w=[f"h2.{k}"])

        ngrp = NFC // 4
        for g in range(ngrp):
            wdi = g % 2
            ubi = g % 2
            wd_g = wd[wdi][:, :].rearrange("p (c d) -> p c d", c=4)
            for c4 in range(4):
                C.dma("pool", wd_g[:, c4, :], wd_v[:, g * 4 + c4, :], f"wd{wdi}.{c4}",
                      w=[f"wd{wdi}.{c4}"])
            for half in range(2):
                pair = g * 2 + half
                wbi = pair % 2
                f0 = (g * 4 + half * 2) * 128
                for kq in range(4):
                    C.dma("pool", wb[wbi][:, 4 * kq:4 * kq + 4, 0:256],
                          wg_v[:, 4 * kq:4 * kq + 4, f0:f0 + 256], f"wb{wbi}.{kq}",
                          w=[f"wb{wbi}.{kq}"])
                for kq in range(4):
                    C.dma("pool", wb[wbi][:, 4 * kq:4 * kq + 4, 256:512],
                          wu_v[:, 4 * kq:4 * kq + 4, f0:f0 + 256], f"wbu{wbi}.{kq}",
                          w=[f"wbu{wbi}.{kq}"])
                for cl in range(2):
                    c = g * 4 + half * 2 + cl
                    c4 = half * 2 + cl
                    ab = abuf[c % 2]
                    an = f"a{c % 2}"
                    for tt in range(3):
                        for k in range(16):
                            C.mm(ps[tt][:, 0:TT], wb[wbi][:, k, cl * 128:(cl + 1) * 128],
                                 h2[:, k, tt * TT:(tt + 1) * TT], start=(k == 0), stop=(k == 15),
                                 r=[f"wb{wbi}.{k // 4}", f"h2.{k}"], w=[f"ps{tt}"])
                        C.copy("act", ab[:, tt * TT:(tt + 1) * TT], ps[tt][:, 0:TT],
                               r=[f"ps{tt}"], w=[an + f".{tt}"])
                    for tt in range(3):
                        for k in range(16):
                            C.mm(ps[3 + tt][:, 0:TT],
                                 wb[wbi][:, k, 256 + cl * 128:256 + (cl + 1) * 128],
                                 h2[:, k, tt * TT:(tt + 1) * TT], start=(k == 0), stop=(k == 15),
                                 r=[f"wbu{wbi}.{k // 4}", f"h2.{k}"], w=[f"ps{3 + tt}"])
                    areads = [an + f".{tt}" for tt in range(3)]
                    C.ts("dve", tbuf[:, :], ab[:, 2:1026], cw[:, 3 * c + 2:3 * c + 3],
                         cb[:, c:c + 1], ALU.mult, ALU.add, r=areads + ["cw", "cb"], w=["tbuf"])
                    C.stt(tbuf[:, :], ab[:, 1:1025], cw[:, 3 * c + 1:3 * c + 2], tbuf[:, :],
                          ALU.mult, ALU.add, r=areads + ["cw", "tbuf"], w=["tbuf"])
                    C.stt(tbuf[:, :], ab[:, 0:1024], cw[:, 3 * c:3 * c + 1], tbuf[:, :],
                          ALU.mult, ALU.add, r=areads + ["cw", "tbuf"], w=["tbuf"])
                    C.act(sbuf_[:, :], tbuf[:, :], AF.Silu, r=["tbuf"], w=["sbuf"])
                    for tt in range(3):
                        lo = max(tt * TT, 2)
                        hi = (tt + 1) * TT
                        C.tt("dve", ub[ubi][:, c4, lo - 2:hi - 2], ps[3 + tt][:, lo - tt * TT:TT],
                             sbuf_[:, lo - 2:hi - 2], ALU.mult,
                             r=[f"ps{3 + tt}", "sbuf"], w=[f"ub{ubi}.{c4}.{tt}"])
            for dc in range(16):
                for t2 in range(2):
                    bank = 6 + (dc * 2 + t2) % 2
                    for c4 in range(4):
                        C.mm(ps[bank][:, :], wd_g[:, c4, dc * 128:(dc + 1) * 128],
                             ub[ubi][:, c4, t2 * 512:(t2 + 1) * 512],
                             start=(c4 == 0), stop=(c4 == 3),
                             r=[f"wd{wdi}.{c4}"] +
                               [f"ub{ubi}.{c4}.{tt}" for tt in range(3)],
                             w=[f"ps{bank}"])
                    xs = x1[:, dc, 2 + t2 * 512:2 + (t2 + 1) * 512]
                    C.tt("dve", xs, ps[bank][:, :], xs, ALU.add,
                         r=[f"ps{bank}", f"x1.{dc}"], w=[f"x1.{dc}"])

        for tt in range(3):
            cols = slice(tt * TT, (tt + 1) * TT)
            rmsnorm_stats(tt)
            for k in range(16):
                C.stt(x1[:, k, cols], x1[:, k, cols], gf[:, k:k + 1], rstd[:, cols],
                      ALU.mult, ALU.mult, r=[f"x1.{k}", "gf", f"rstd.{tt}"], w=[f"y.{k}.{tt}"])
        outs = []
        for q in range(4):
            outs.append(C.dma("sp", yT_v[:, 4 * q:4 * q + 4, :], x1[:, 4 * q:4 * q + 4, 2:1026],
                              f"yout.{q}",
                              r=[f"y.{k}.{tt}" for k in range(4 * q, 4 * q + 4) for tt in range(3)]))
        S.wait_all("sp", outs)
        S.emit(block)
    return nc


class _Stop(Exception):
    pass


def build_A(NT=8, do_hgrn=True, do_fox=True, do_fb=True, stop=0):
    def chk(level):
        if stop == level:
            raise _Stop()
    LA = 512 * NT
    nc = bass.Bass("TRN2", target_bir_lowering=False)
    din = lambda n, sh, dt: nc.dram_tensor(n, sh, dt, kind="ExternalInput").ap()
    xT = din("xT", [D, LA], F32)
    wfm = din("wfm", [D, 1280], F32)
    wtm = din("wtm", [D, 512], F32)
    wfb = din("wfb", [D, 64], F32)
    g1d = din("g1", [128, 16], F32)
    lbpd = din("lbp", [128, 4], F32)
    hngd = din("hng", [128, 1], F32)
    fgbd = din("fgb", [64, 1], F32)
    onesd = din("ones", [128, 128], BF16)
    identd = din("ident", [128, 128], BF16)
    hmaskd = din("hmask", [128, 512], F32)
    fmaskd = din("fmaskb", [128, 128], BF16)
    rmaskd = din("rmask", [128, 512], F32)
    negoned = din("negone", [64, 2], F32)
    seld = din("sel", [128, 256], BF16)
    identfd = din("identf", [64, 64], F32)
    oTd = nc.dram_tensor("oT", [512, LA], BF16, kind="ExternalOutput").ap()

    xT_v = xT.rearrange("(k p) t -> p k t", p=128)
    wfm_v = wfm.rearrange("(k p) c -> p k c", p=128)
    wtm_v = wtm.rearrange("(k p) c -> p k c", p=128)
    wfb_v = wfb.rearrange("(k p) c -> p k c", p=128)
    oT_v = oTd.rearrange("(h p) t -> p h t", p=128)
    SC = 128.0 ** -0.5

    with contextlib.ExitStack() as st:
        sb = lambda name, shape, dt: st.enter_context(nc.sbuf_tensor(name, shape, dt))
        wfm_s = sb("wfm_s", [128, 16, 1280], BF16)
        wtm_s = sb("wtm_s", [128, 16, 512], BF16)
        wfb_s = sb("wfb_s", [128, 16, 64], BF16)
        xt = [sb(f"xt{i}", [128, 16, 256], F32) for i in range(2)]
        sq = sb("sq", [128, 16, 256], BF16)
        hT = sb("hT", [128, 16, 512], BF16)
        rstd1 = sb("rstd1", [128, 512], F32)
        rtmp = sb("rtmp", [128, 512], F32)
        KbT = sb("KbT", [128, 2, L], BF16)
        Vb = sb("Vb", [128, 32, 256], BF16)
        QbT = sb("QbT", [128, 2, 512], BF16)
        A = sb("A", [128, 512], F32)
        B = sb("B", [128, 512], F32)
        Cb = sb("Cb", [128, 512], F32)
        E1 = sb("E1", [128, 2, 512], F32)
        qdT = sb("qdT", [128, 2, 512], BF16)
        kdT = sb("kdT", [128, 2, 512], BF16)
        keT = sb("keT", [128, 2, 512], BF16)
        sgT = sb("sgT", [128, 2, 512], BF16)
        Va = sb("Va", [128, 4, 256], BF16)
        ke_tm = [sb(f"ke_tm{i}", [128, 4, 256], BF16) for i in range(2)]
        attnT = sb("attnT", [128, 512], BF16)
        S32 = sb("S32", [128, 2, 128], F32)
        Sbf = sb("Sbf", [128, 8, 128], BF16)
        osq = sb("osq", [128, 512], BF16)
        crow = sb("crow", [64, 2, 512], F32)
        crow_bf = sb("crow_bf", [128, 512], BF16)
        sel = sb("sels", [128, 2, 128], BF16)
        lsg = sb("lsg", [64, 512], F32)
        nck = sb("nck", [128, 32, 2, 1], F32)
        identf = sb("identfs", [64, 64], F32)
        pT = [sb(f"pT{i}", [128, 512], BF16) for i in range(3)]
        oTt = sb("oTt", [128, 4, 512], BF16)
        onesf = sb("onesf", [64, 512], F32)
        g1 = sb("g1s", [128, 16], F32)
        lbp = sb("lbps", [128, 4], F32)
        lb = sb("lb", [128, 2], F32)
        oml = sb("oml", [128, 2], F32)
        noml = sb("noml", [128, 2], F32)
        hng = sb("hngs", [128, 1], F32)
        fgb = sb("fgbs", [64, 1], F32)
        ones = sb("oness", [128, 128], BF16)
        ident = sb("idents", [128, 128], BF16)
        hmask = sb("hmasks", [128, 512], F32)
        fmaskb = sb("fmasks", [128, 128], BF16)
        rmask = sb("rmasks", [128, 512], F32)
        negone = sb("negones", [64, 2], F32)
        ps = [st.enter_context(nc.psum_tensor(f"ps{i}", [128, 512], F32)) for i in range(7)]
        psT = st.enter_context(nc.psum_tensor("psT", [128, 1024], BF16))
        S = Sched(nc, st)
        C = Ctx(S)
        block = st.enter_context(nc.Block())

        for (t_, d_, n_) in ((g1, g1d, "g1"), (lbp, lbpd, "lbp"), (hng, hngd, "hng"),
                             (fgb, fgbd, "fgb"), (ones, onesd, "ones"), (ident, identd, "ident"),
                             (hmask, hmaskd, "hmask"), (fmaskb, fmaskd, "fmaskb"),
                             (rmask, rmaskd, "rmask"), (negone, negoned, "negone"),
                             (identf, identfd, "identf")):
            C.dma("sp", t_[:, :], d_, n_, w=[n_])
        C.dma("sp", sel[:, :, :], seld.rearrange("p (h m) -> p h m", h=2), "sel", w=["sel"])
        for kq in range(4):
            C.dma("pool", wfm_s[:, 4 * kq:4 * kq + 4, :], wfm_v[:, 4 * kq:4 * kq + 4, :],
                  f"wfm.{kq}", w=[f"wfm.{kq}"])
        C.dma("pool", wfb_s[:, :, :], wfb_v, "wfb", w=["wfb"])
        for kq in range(4):
            C.dma("pool", wtm_s[:, 4 * kq:4 * kq + 4, :], wtm_v[:, 4 * kq:4 * kq + 4, :],
                  f"wtm.{kq}", w=[f"wtm.{kq}"])
        C.memset("pool", onesf[:, :], 1.0, w=["onesf"])
        C.memset("pool", S32[:, :, :], 0.0, w=["S32.0", "S32.1"])
        for rr in range(2):
            C.memset("pool", ke_tm[rr][:, :, :], 0.0, w=[f"ke_tm{rr}.0", f"ke_tm{rr}.1"])
        C.memset("pool", crow_bf[:, :], 0.0, w=["crow_bf"])
        C.tt("dve", lb[:, :], lbp[:, 0:2], lbp[:, 2:4], ALU.subtract, r=["lbp"], w=["lb"])
        C.act(lb[:, :], lb[:, :], AF.Sigmoid, r=["lb"], w=["lb"])
        C.ts("dve", oml[:, :], lb[:, :], -1.0, 1.0, ALU.mult, ALU.add, r=["lb"], w=["oml"])
        C.ts("dve", noml[:, :], lb[:, :], 1.0, -1.0, ALU.mult, ALU.add, r=["lb"], w=["noml"])

        nproj = [0]
        stores = []
        wfm_r = [f"wfm.{kq}" for kq in range(4)]
        wtm_r = [f"wtm.{kq}" for kq in range(4)]

        try:
          chk(1)
          for i in range(NT):
            t0 = i * 512
            cr = i % 2
            for half in range(2):
                hc = slice(half * 256, (half + 1) * 256)
                xb = xt[half]
                xn = f"xt{half}"
                C.dma("sp", xb[:, :, :], xT_v[:, :, t0 + half * 256:t0 + (half + 1) * 256], xn,
                      w=[xn])
                C.act(sq[:, :, :], xb[:, :, :], AF.Square, r=[xn], w=["sq"])
                sb_ = 2 if half == 0 else 6
                for k in range(16):
                    C.mm(ps[sb_][:, 0:256], ones[:, :], sq[:, k, :], start=(k == 0), stop=(k == 15),
                         r=["ones", "sq"], w=[f"ps{sb_}"])
                rstd_from_ss(C, ps[sb_][:, 0:256], rtmp[:, hc], rstd1[:, hc], 1.0 / D,
                             r=[f"ps{sb_}"], wt=f"rtmp.{half}", wo=f"rstd1.{half}")
                for k in range(16):
                    C.stt(hT[:, k, hc], xb[:, k, :], g1[:, k:k + 1], rstd1[:, hc],
                          ALU.mult, ALU.mult, r=[xn, "g1", f"rstd1.{half}"], w=[f"hT.{half}.{k}"])
            hT_r = None
            chk(2)

            def proj_fm(c, M=128, wsrc=None):
                bank = nproj[0] % 2
                nproj[0] += 1
                for k in range(16):
                    lhsT = wfm_s[:, k, c * 128:(c + 1) * 128] if wsrc is None else wsrc[:, k, :]
                    C.mm(ps[bank][0:M, :], lhsT, hT[:, k, :], start=(k == 0), stop=(k == 15),
                         r=(wfm_r if wsrc is None else ["wfb"]) + [f"hT.0.{k}", f"hT.1.{k}"],
                         w=[f"ps{bank}"])
                return bank

            for hh in range(2):
                bank = proj_fm(hh)
                pn = f"ps{bank}"
                C.act(A[:, :], ps[bank][:, :], AF.Sigmoid, r=[pn], w=["A"])
                C.ts("dve", B[:, :], A[:, :], oml[:, hh:hh + 1], lb[:, hh:hh + 1], ALU.mult, ALU.add,
                     r=["A", "oml", "lb"], w=["B"])
                C.ts("pool", Cb[:, :], A[:, :], noml[:, hh:hh + 1], oml[:, hh:hh + 1],
                     ALU.mult, ALU.add, r=["A", "oml", "noml"], w=["Cb"])
                C.act(B[:, :], B[:, :], AF.Ln, r=["B"], w=["B"])
                C.scan(A[:, :], rmask[:, :], B[:, :], 0.0, ALU.mult, ALU.add,
                       r=["rmask", "B"], w=["A"])
                C.act(E1[:, hh, :], A[:, :], AF.Exp, r=["A"], w=[f"E1.{hh}"])
                C.act(B[:, :], A[:, :], AF.Exp, r=["A"], w=["B"], scale=-1.0)
                C.tt("dve", Cb[:, :], Cb[:, :], B[:, :], ALU.mult, r=["Cb", "B"], w=["Cb"])
                C.copy("pool", kdT[:, hh, :], Cb[:, :], r=["Cb"], w=[f"kdT.{hh}"])
                for n in range(8):
                    C.ts("pool", keT[:, hh, 64 * n:64 * n + 64], Cb[:, 64 * n:64 * n + 64],
                         E1[:, hh, 64 * n + 63:64 * n + 64], 1.0, ALU.mult, ALU.mult,
                         r=["Cb", f"E1.{hh}"], w=[f"keT.{hh}.{n}"])
            chk(3)
            for hh in range(2):
                bank = proj_fm(2 + hh)
                C.stt(qdT[:, hh, :], ps[bank][:, :], SC, E1[:, hh, :], ALU.mult, ALU.mult,
                      r=[f"ps{bank}", f"E1.{hh}"], w=[f"qdT.{hh}"])
            for hh in range(2):
                bank = proj_fm(4 + hh)
                C.act(sgT[:, hh, :], ps[bank][:, :], AF.Silu, r=[f"ps{bank}"], w=[f"sgT.{hh}"])
            for hh in range(2):
                bank = proj_fm(6 + hh)
                C.act(QbT[:, hh, :], ps[bank][:, :], AF.Copy, r=[f"ps{bank}"], w=[f"QbT.{hh}"],
                      scale=SC)
            for hh in range(2):
                bank = proj_fm(8 + hh)
                C.copy("dve", KbT[:, hh, t0:t0 + 512], ps[bank][:, :], r=[f"ps{bank}"],
                       w=[f"KbT.{hh}.{i}"])
            chk(4)
            if do_fb:
                bank = proj_fm(0, M=64, wsrc=wfb_s)
                C.act(lsg[:, :], ps[bank][0:64, :], AF.Sigmoid, r=[f"ps{bank}", "fgb"], w=["lsg"],
                      bias=fgb[:, 0:1])
                C.act(lsg[:, :], lsg[:, :], AF.Ln, r=["lsg"], w=["lsg"])
                init = 0.0 if i == 0 else crow[:, 1 - cr, 511:512]
                C.scan(crow[:, cr, :], onesf[:, :], lsg[:, :], init, ALU.mult, ALU.add,
                       r=["onesf", "lsg", f"crow.{1 - cr}"], w=[f"crow.{cr}"])
                C.copy("pool", crow_bf[0:64, :], crow[:, cr, :], r=[f"crow.{cr}"], w=["crow_bf"])
                for tb in range(4):
                    C.tr(ps[2][:, tb * 64:(tb + 1) * 64], crow[:, cr, tb * 128:(tb + 1) * 128],
                         identf[:, :], r=[f"crow.{cr}", "identf"], w=["ps2"])
                C.ts("dve", nck[:, 4 * i:4 * i + 4, :, :],
                     ps[2][:, 0:256].rearrange("p (b h x) -> p b h x", b=4, h=2)[:, :, :, 0:1],
                     -1.0, None, ALU.mult, None, r=["ps2"], w=[f"nck.{i}"])
            chk(5)
            for tb in range(4):
                bank = nproj[0] % 2
                nproj[0] += 1
                for k in range(16):
                    C.mm(ps[bank][:, :], hT[:, k, tb * 128:(tb + 1) * 128], wtm_s[:, k, :],
                         start=(k == 0), stop=(k == 15),
                         r=wtm_r + [f"hT.0.{k}", f"hT.1.{k}"], w=[f"ps{bank}"])
                C.copy("act", Va[:, tb, :], ps[bank][:, 0:256], r=[f"ps{bank}"], w=[f"Va.{tb}"])
                C.copy("dve", Vb[:, 4 * i + tb, :], ps[bank][:, 256:512], r=[f"ps{bank}"],
                       w=[f"Vb.{4 * i + tb}"])

            chk(6)
            for hh in (range(2) if do_hgrn else ()):
                hs = slice(hh * 128, (hh + 1) * 128)
                for tb in range(4):
                    bs = slice(tb * 128, (tb + 1) * 128)
                    C.tr(psT[:, bs], keT[:, hh, bs], ident[:, :],
                         r=[f"keT.{hh}.{2 * tb}", f"keT.{hh}.{2 * tb + 1}", "ident"], w=["psT"])
                for rr in range(2):
                    C.copy("act", ke_tm[rr][64 * rr:64 * rr + 64, :, hs],
                           psT[64 * rr:64 * rr + 64, 0:512].rearrange("p (b k) -> p b k", b=4),
                           r=["psT"], w=[f"ke_tm{rr}.{hh}"])
                chk(7)
                for tb in range(4):
                    bs = slice(tb * 128, (tb + 1) * 128)
                    C.mm(ps[3][:, bs], kdT[:, hh, bs], qdT[:, hh, bs], start=True, stop=True,
                         r=[f"kdT.{hh}", f"qdT.{hh}"], w=["ps3"])
                C.tt("dve", attnT[:, :], ps[3][:, :], hmask[:, :], ALU.mult,
                     r=["ps3", "hmask"], w=["attnT"])
                chk(8)
                for n in range(8):
                    tb, rr = n // 2, n % 2
                    pb = 4 if n < 4 else 1
                    C.mm(ps[pb][:, (n % 4) * 128:(n % 4 + 1) * 128],
                         ke_tm[rr][:, tb, hs], Va[:, tb, hs],
                         start=True, stop=True, r=[f"ke_tm{rr}.{hh}", f"Va.{tb}"], w=[f"ps{pb}"])
                chk(9)
                for n in range(8):
                    pb = 4 if n < 4 else 1
                    C.copy("act", Sbf[:, n, :], S32[:, hh, :], r=[f"S32.{hh}"], w=[f"Sbf.{n}"])
                    C.stt(S32[:, hh, :], S32[:, hh, :], E1[:, hh, 64 * n + 63:64 * n + 64],
                          ps[pb][:, (n % 4) * 128:(n % 4 + 1) * 128], ALU.mult, ALU.add,
                          r=[f"S32.{hh}", f"E1.{hh}", f"ps{pb}"], w=[f"S32.{hh}"])
                chk(10)
                for tb in range(4):
                    bs = slice(tb * 128, (tb + 1) * 128)
                    C.mm(ps[5][:, bs], Va[:, tb, hs], attnT[:, bs], start=True, stop=False,
                         r=[f"Va.{tb}", "attnT"], w=["ps5"])
                    for rr in range(2):
                        n = 2 * tb + rr
                        C.mm(ps[5][:, n * 64:(n + 1) * 64], Sbf[:, n, :],
                             qdT[:, hh, n * 64:(n + 1) * 64], start=False, stop=(rr == 1),
                             r=[f"Sbf.{n}", f"qdT.{hh}"], w=["ps5"])
                chk(11)
                C.act(osq[:, :], ps[5][:, :], AF.Square, r=["ps5"], w=["osq"])
                C.mm(ps[6][:, :], ones[:, :], osq[:, :], start=True, stop=True, r=["ones", "osq"],
                     w=["ps6"])
                rstd_from_ss(C, ps[6][:, :], A[:, :], B[:, :], 1.0 / 128, r=["ps6"], wt="A", wo="B")
                C.tt("dve", Cb[:, :], ps[5][:, :], B[:, :], ALU.mult, r=["ps5", "B"], w=["Cb"])
                C.stt(oTt[:, hh, :], Cb[:, :], hng[:, 0:1], sgT[:, hh, :], ALU.mult, ALU.mult,
                      r=["Cb", "hng", f"sgT.{hh}"], w=[f"oTt.{hh}"])

            cnt = 0
            for hh in (range(2) if do_fox else ()):
                hs = slice(hh * 128, (hh + 1) * 128)
                nkb = 4 * i + 4
                for kb in range(nkb):
                    r_ = kb - 4 * i
                    q0 = 128 * r_ if r_ > 0 else 0
                    bank = 3 + cnt % 2
                    pt = pT[cnt % 3]
                    ptn = f"pT{cnt % 3}"
                    cnt += 1
                    bn = [f"ps{bank}"]
                    C.mm(ps[bank][:, q0:512], KbT[:, hh, kb * 128:(kb + 1) * 128], QbT[:, hh, q0:512],
                         start=True, stop=False, r=[f"KbT.{hh}.{kb // 4}", f"QbT.{hh}"], w=bn)
                    C.mm(ps[bank][:, q0:512], sel[:, hh, :], crow_bf[:, q0:512], start=False,
                         stop=(r_ < 0), r=["sel", "crow_bf"], w=bn)
                    if r_ >= 0:
                        C.mm(ps[bank][:, q0:q0 + 128], ident[:, :], fmaskb[:, :], start=False,
                             stop=True, r=["ident", "fmaskb"], w=bn)
                    C.act(pt[:, q0:512], ps[bank][:, q0:512], AF.Exp,
                          r=bn + [f"nck.{kb // 4}"], w=[ptn], bias=nck[:, kb, hh, :])
                    C.mm(ps[5][:, q0:512], Vb[:, kb, hs], pt[:, q0:512], start=(kb == 0),
                         stop=(kb == nkb - 1), r=[f"Vb.{kb}", ptn], w=["ps5"])
                    C.mm(ps[6][:, q0:512], ones[:, :], pt[:, q0:512], start=(kb == 0),
                         stop=(kb == nkb - 1), r=["ones", ptn], w=["ps6"])
                C.recip(A[:, :], ps[6][:, :], r=["ps6"], w=["A"])
                C.tt("dve", oTt[:, 2 + hh, :], ps[5][:, :], A[:, :], ALU.mult, r=["ps5", "A"],
                     w=[f"oTt.{2 + hh}"])
            stores.append(C.dma("sp", oT_v[:, :, t0:t0 + 512], oTt[:, :, :], f"ost",
                                r=[f"oTt.{h}" for h in range(4)]))
        except _Stop:
            pass
        S.wait_all("sp", stores)
        S.emit(block)
    return nc


def _pk(v):
    v = np.asarray(v, np.float32)
    return np.ascontiguousarray(v.reshape(-1, 128).T)


def run_phase_B(x, oT_full, inputs):
    ncB = build_B()
    ones = np.ones((128, 128), dtype=ml_dtypes.bfloat16)
    cw = np.ascontiguousarray(
        np.asarray(inputs["ffn_conv_w"][0], np.float32).T.reshape(NFC, 128, 3).transpose(1, 0, 2)
    ).reshape(128, NFC * 3)
    common = dict(
        w_out=np.ascontiguousarray(inputs["w_out"][0], dtype=np.float32),
        g2=_pk(inputs["norm2_gain"][0]), gf=_pk(inputs["final_norm_gain"]),
        w_gate=np.ascontiguousarray(inputs["w_ffn_gate"][0], dtype=np.float32),
        w_up=np.ascontiguousarray(inputs["w_ffn_up"][0], dtype=np.float32),
        w_down=np.ascontiguousarray(inputs["w_ffn_down"][0], dtype=np.float32),
        cw=cw, cb=_pk(inputs["ffn_conv_b"][0]), ones=ones)
    in_maps = []
    for c in range(8):
        b, j = c // 4, c % 4
        xT = np.zeros((D, TB), np.float32)
        oTc = np.zeros((D, TB), ml_dtypes.bfloat16)
        lo = j * 1024 - 2
        src_lo = max(lo, 0)
        xT[:, src_lo - lo:] = x[b, src_lo:j * 1024 + 1024, :].T
        oTc[:, src_lo - lo:] = oT_full[b][:, src_lo:j * 1024 + 1024]
        m = dict(common)
        m["xT"] = xT
        m["oT"] = oTc
        in_maps.append(m)
    res = run_bass_kernel_spmd(ncB, in_maps, core_ids=list(range(8)))
    out = np.empty((2, L, D), np.float32)
    for c in range(8):
        b, j = c // 4, c % 4
        out[b, j * 1024:(j + 1) * 1024, :] = np.asarray(res.results[c]["yT"]).T
    return out


def run_phase_A(x, inputs):
    ncA = build_A()
    W = np.asarray(inputs["w_in"][0], np.float32)
    bf = ml_dtypes.bfloat16
    ones = np.ones((128, 128), dtype=bf)
    ident = np.eye(128, dtype=np.float32).astype(bf)
    idx = np.arange(128)
    same = (idx[:, None] // 64) == (idx[None, :] // 64)
    hmask = np.tile((same & (idx[:, None] <= idx[None, :])).astype(np.float32), (1, 4))
    fmaskb = np.where(idx[None, :] >= idx[:, None], 0.0, -30000.0).astype(np.float32).astype(bf)
    rmask = np.ones((128, 512), np.float32)
    rmask[:, 0::64] = 0.0
    negone = np.full((64, 2), -1.0, np.float32)
    sel = np.zeros((128, 256), np.float32)
    sel[0, 0:128] = 1.0
    sel[32, 128:256] = 1.0
    sel = sel.astype(bf)
    lbpar = np.asarray(inputs["hgrn_lb_param"], np.float32)
    hng = np.asarray(inputs["hgrn_norm_gain"][0], np.float32).reshape(128, 1)
    fgbias = np.asarray(inputs["fox_gate_bias"][0], np.float32)
    g1 = _pk(inputs["norm1_gain"][0])
    in_maps = []
    for c in range(8):
        b, p = c // 4, c % 4
        cs = slice(256 * p, 256 * p + 256)
        grp = lambda off: W[:, off + 256 * p: off + 256 * p + 256]
        wfm = np.ascontiguousarray(np.concatenate(
            [grp(1024), grp(0), grp(3072), grp(4096), grp(5120)], axis=1))
        wtm = np.ascontiguousarray(np.concatenate([grp(2048), grp(6144)], axis=1))
        wfb = np.zeros((D, 64), np.float32)
        wfb[:, 0] = W[:, 7168 + 2 * p]
        wfb[:, 32] = W[:, 7168 + 2 * p + 1]
        lbp = np.stack([lbpar[0, 256 * p:256 * p + 128], lbpar[0, 256 * p + 128:256 * p + 256],
                        lbpar[1, 256 * p:256 * p + 128], lbpar[1, 256 * p + 128:256 * p + 256]],
                       axis=1).astype(np.float32)
        fgb = np.zeros((64, 1), np.float32)
        fgb[0, 0] = fgbias[2 * p]
        fgb[32, 0] = fgbias[2 * p + 1]
        in_maps.append(dict(
            xT=np.ascontiguousarray(x[b].T), wfm=wfm, wtm=wtm, wfb=wfb, g1=g1,
            lbp=np.ascontiguousarray(lbp), hng=hng, fgb=fgb, ones=ones, ident=ident,
            hmask=hmask, fmaskb=fmaskb, rmask=rmask, negone=negone, sel=sel,
            identf=np.eye(64, dtype=np.float32)))
    res = run_bass_kernel_spmd(ncA, in_maps, core_ids=list(range(8)))
    oT_full = [np.zeros((D, L), dtype=bf) for _ in range(2)]
    for c in range(8):
        b, p = c // 4, c % 4
        o = np.asarray(res.results[c]["oT"])
        oT_full[b][256 * p:256 * p + 256] = o[0:256]
        oT_full[b][1024 + 256 * p:1024 + 256 * p + 256] = o[256:512]
    return oT_full


def kernel(**inputs):
    x = np.asarray(inputs["x"], np.float32)
    oT_full = run_phase_A(x, inputs)
    return run_phase_B(x, oT_full, inputs)
```
